# Optimizing a Trainium2 kernel written in Bass

```python
import jax, jax.numpy as jnp
from jax import lax
import numpy as np

D_MODEL = 1024
BATCH = 2
SEQ = 8192
DEPTH = 2

CONV_CHANNELS = 512
CONV_GROUPS = 8
CONV_WIDTH = 3
N_Q_HEADS = 8
N_KV_HEADS = 2
HEAD_DIM = 64
ATTN_WIDTH = N_Q_HEADS * HEAD_DIM
WINDOW = 128
BLOCK = 128
MIX_WIDTH = CONV_CHANNELS + ATTN_WIDTH
IN_COLS = 3 * CONV_CHANNELS + (N_Q_HEADS + 2 * N_KV_HEADS) * HEAD_DIM
D_FF = -((-8 * D_MODEL) // (3 * 256)) * 256
EPS = 1e-6
NEG_INF = -1e30

kernel_name = "hymba_style_conv_swa_sink_hybrid"


def rms_norm(x, g):
    xf = x.astype(jnp.float32)
    y = xf * lax.rsqrt(jnp.mean(xf * xf, axis=-1, keepdims=True) + EPS)
    return (y * g.astype(jnp.float32)).astype(x.dtype)


def short_gated_conv(b_gate, c_gate, h, conv_w):
    seq = h.shape[1]
    u = c_gate * h
    up = jnp.pad(u, ((0, 0), (CONV_WIDTH - 1, 0), (0, 0)))
    y = conv_w[0] * up[:, 0:seq]
    for tap in range(1, CONV_WIDTH):
        y = y + conv_w[tap] * up[:, tap:tap + seq]
    return b_gate * y


def band_keys(t, nb):
    b = t.shape[0]
    tb = t.reshape(b, nb, BLOCK, t.shape[2], t.shape[3])
    prev = jnp.pad(tb[:, :-1], ((0, 0), (1, 0), (0, 0), (0, 0), (0, 0)))
    return jnp.concatenate([prev, tb], axis=2)


def sliding_window_attention_with_sinks(q, k, v, sinks):
    b, seq = q.shape[0], q.shape[1]
    nb = seq // BLOCK
    grp = N_Q_HEADS // N_KV_HEADS
    qb = q.reshape(b, nb, BLOCK, N_KV_HEADS, grp, HEAD_DIM)
    kb = band_keys(k, nb)
    vb = band_keys(v, nb)
    scale = HEAD_DIM ** -0.5
    s = jnp.einsum('bnqhgd,bnkhd->bnhgqk', qb, kb).astype(jnp.float32) * scale
    qpos = jnp.arange(nb)[:, None] * BLOCK + jnp.arange(BLOCK)[None, :]
    kpos = (jnp.arange(nb)[:, None] - 1) * BLOCK + jnp.arange(2 * BLOCK)[None, :]
    diff = qpos[:, :, None] - kpos[:, None, :]
    valid = (diff >= 0) & (diff < WINDOW) & (kpos[:, None, :] >= 0)
    s = jnp.where(valid[None, :, None, None], s, NEG_INF)
    sink = sinks.astype(jnp.float32).reshape(N_KV_HEADS, grp)[None, None, :, :, None, None]
    m = jnp.maximum(jnp.max(s, axis=-1, keepdims=True), sink)
    p = jnp.exp(s - m)
    denom = jnp.sum(p, axis=-1, keepdims=True) + jnp.exp(sink - m)
    probs = (p / denom).astype(v.dtype)
    o = jnp.einsum('bnhgqk,bnkhd->bnqhgd', probs, vb)
    return o.reshape(b, seq, ATTN_WIDTH)


def setup_inputs(seed: int = 0) -> dict:
    key = jax.random.key(seed)
    ks = jax.random.split(key, 16)
    f32 = jnp.float32

    def gain(k, shape):
        return 1.0 + 0.02 * jax.random.normal(k, shape, f32)

    return {
        "x": jax.random.normal(ks[0], (BATCH, SEQ, D_MODEL), f32),
        "norm1_g": gain(ks[1], (DEPTH, D_MODEL)),
        "w_in": jax.random.normal(ks[2], (DEPTH, D_MODEL, IN_COLS), f32) * D_MODEL ** -0.5,
        "conv_w": jax.random.normal(ks[3], (DEPTH, CONV_WIDTH, CONV_CHANNELS), f32) * CONV_WIDTH ** -0.5,
        "q_norm_g": gain(ks[4], (DEPTH, HEAD_DIM)),
        "k_norm_g": gain(ks[5], (DEPTH, HEAD_DIM)),
        "sinks": 0.5 * jax.random.normal(ks[6], (DEPTH, N_Q_HEADS), f32),
        "conv_out_g": gain(ks[7], (DEPTH, CONV_CHANNELS)),
        "attn_out_g": gain(ks[8], (DEPTH, ATTN_WIDTH)),
        "w_o": jax.random.normal(ks[9], (DEPTH, MIX_WIDTH, D_MODEL), f32) * MIX_WIDTH ** -0.5,
        "norm2_g": gain(ks[10], (DEPTH, D_MODEL)),
        "w_gate": jax.random.normal(ks[11], (DEPTH, D_MODEL, D_FF), f32) * D_MODEL ** -0.5,
        "w_up": jax.random.normal(ks[12], (DEPTH, D_MODEL, D_FF), f32) * D_MODEL ** -0.5,
        "w_down": jax.random.normal(ks[13], (DEPTH, D_FF, D_MODEL), f32) * D_FF ** -0.5,
    }


def reference(x, norm1_g, w_in, conv_w, q_norm_g, k_norm_g, sinks, conv_out_g,
              attn_out_g, w_o, norm2_g, w_gate, w_up, w_down):
    b, seq = x.shape[0], x.shape[1]
    c = CONV_CHANNELS
    o_q = 3 * c
    o_k = o_q + ATTN_WIDTH
    o_v = o_k + N_KV_HEADS * HEAD_DIM
    for l in range(DEPTH):
        h = rms_norm(x, norm1_g[l])
        proj = h @ w_in[l]
        b_gate = proj[..., 0:c]
        c_gate = proj[..., c:2 * c]
        hc = proj[..., 2 * c:3 * c]
        q = proj[..., o_q:o_k].reshape(b, seq, N_Q_HEADS, HEAD_DIM)
        k = proj[..., o_k:o_v].reshape(b, seq, N_KV_HEADS, HEAD_DIM)
        v = proj[..., o_v:].reshape(b, seq, N_KV_HEADS, HEAD_DIM)

        conv_out = short_gated_conv(b_gate, c_gate, hc, conv_w[l])

        q = rms_norm(q, q_norm_g[l])
        k = rms_norm(k, k_norm_g[l])
        attn_out = sliding_window_attention_with_sinks(q, k, v, sinks[l])

        mix = jnp.concatenate([rms_norm(conv_out, conv_out_g[l]),
                               rms_norm(attn_out, attn_out_g[l])], axis=-1)
        x = x + mix @ w_o[l]

        h2 = rms_norm(x, norm2_g[l])
        x = x + (jax.nn.silu(h2 @ w_gate[l]) * (h2 @ w_up[l])) @ w_down[l]
    return x
```

```python
import numpy as np
import ml_dtypes
from contextlib import ExitStack
import concourse.bass as bass
import concourse.mybir as mybir
from concourse.bass_utils import run_bass_kernel_spmd

F32 = mybir.dt.float32
BF16 = mybir.dt.bfloat16
AF = mybir.ActivationFunctionType
ALU = mybir.AluOpType
AX = mybir.AxisListType

D = 1024
INC = 2304
DFF = 2816
NCH = DFF // 128
PARTS = [(0, 4), (4, 8), (8, 12), (12, 16), (16, 19), (19, 22)]
EPS = 1e-6
NCORES = 8
TOK_PER_CORE = 2048
NHALO = 2


class Sched:
    ENGS = ("pe", "act", "dve", "pool", "sp")

    def __init__(self):
        self.ops = []
        self.last_write = {}
        self.readers = {}

    def add(self, eng, fn, reads=(), writes=(), dma=None):
        idx = len(self.ops)
        deps = set()
        for k in reads:
            w = self.last_write.get(k)
            if w is not None:
                deps.add(w)
        for k in writes:
            w = self.last_write.get(k)
            if w is not None:
                deps.add(w)
            for r in self.readers.get(k, ()):
                deps.add(r)
        deps.discard(idx)
        self.ops.append(dict(eng=eng, fn=fn, deps=deps, dma=dma, signal=False))
        for k in reads:
            self.readers.setdefault(k, []).append(idx)
        for k in writes:
            self.last_write[k] = idx
            self.readers[k] = []
        return idx

    def emit(self, nc, final_wait_eng="sp"):
        ops = self.ops
        for i, o in enumerate(ops):
            for d in o["deps"]:
                od = ops[d]
                if od["dma"] is not None:
                    continue
                if od["eng"] == "pe" and o["eng"] == "pe" and o["dma"] is None:
                    continue
                od["signal"] = True
        cnt = {e: 0 for e in self.ENGS}
        dcnt = {}
        for o in ops:
            if o["dma"] is not None:
                dcnt[o["dma"]] = dcnt.get(o["dma"], 0) + 16
                o["val"] = dcnt[o["dma"]]
            elif o["signal"]:
                cnt[o["eng"]] += 1
                o["val"] = cnt[o["eng"]]
        with ExitStack() as st:
            sems = {}
            for e in self.ENGS:
                sems[("eng", e)] = st.enter_context(nc.semaphore("s_" + e))
            for k in dcnt:
                sems[("dma", k)] = st.enter_context(nc.semaphore("d_" + str(k)))
            block = st.enter_context(nc.Block())
            per_eng = {e: [] for e in self.ENGS}
            for i, o in enumerate(ops):
                per_eng[o["eng"]].append(i)

            def run(engname, eng):
                waited = {}
                for i in per_eng[engname]:
                    o = ops[i]
                    need = {}
                    for d in o["deps"]:
                        od = ops[d]
                        if od["dma"] is not None:
                            key = ("dma", od["dma"])
                        else:
                            if od["eng"] == "pe" and engname == "pe" and o["dma"] is None:
                                continue
                            key = ("eng", od["eng"])
                        v = od["val"]
                        if need.get(key, 0) < v:
                            need[key] = v
                    for key, v in need.items():
                        if waited.get(key, 0) >= v:
                            continue
                        eng.wait_ge(sems[key], v)
                        waited[key] = v
                    ins = o["fn"](eng)
                    if o["dma"] is not None:
                        ins.then_inc(sems[("dma", o["dma"])], 16)
                    elif o["signal"]:
                        ins.then_inc(sems[("eng", engname)], 1)
                if engname == final_wait_eng:
                    for k, v in dcnt.items():
                        eng.wait_ge(sems[("dma", k)], v)
                    for e in self.ENGS:
                        if e != engname and cnt[e] > 0:
                            eng.wait_ge(sems[("eng", e)], cnt[e])

            @block.tensor
            def _(e):
                run("pe", e)

            @block.scalar
            def _(e):
                run("act", e)

            @block.vector
            def _(e):
                run("dve", e)

            @block.gpsimd
            def _(e):
                run("pool", e)

            @block.sync
            def _(e):
                run("sp", e)
        return cnt, dcnt


def slot_of(b):
    return b + 2 if b < 0 else 2 + (b % 8)


def hidx_of(b):
    return 0 if b < 0 else 1 + (b % 8)


def build(NL=2):
    from collections import deque
    nc = bass.Bass("TRN2", target_bir_lowering=False)
    x_d = nc.dram_tensor("x", [(16 + NHALO) * 128, D], F32, kind="ExternalInput").ap()
    w_in_d = nc.dram_tensor("w_in", [2, D, INC], F32, kind="ExternalInput").ap()
    w_o_d = nc.dram_tensor("w_o", [2, D, D], F32, kind="ExternalInput").ap()
    w_g_d = nc.dram_tensor("w_gate", [2, D, DFF], F32, kind="ExternalInput").ap()
    w_u_d = nc.dram_tensor("w_up", [2, D, DFF], F32, kind="ExternalInput").ap()
    w_d_d = nc.dram_tensor("w_down", [2, DFF, D], F32, kind="ExternalInput").ap()
    gvec_d = nc.dram_tensor("gvec", [128, 72], F32, kind="ExternalInput").ap()
    rowc_d = nc.dram_tensor("rowc", [128, 272], F32, kind="ExternalInput").ap()
    masks_d = nc.dram_tensor("masks", [128, 3, 128], BF16, kind="ExternalInput").ap()
    ident_d = nc.dram_tensor("ident", [128, 128], BF16, kind="ExternalInput").ap()
    out_d = nc.dram_tensor("out", [TOK_PER_CORE, D], F32, kind="ExternalOutput").ap()

    S = Sched()
    st = ExitStack()

    def sb(name, shape, dt):
        return st.enter_context(nc.sbuf_tensor("sb_" + name, shape, dt))

    def ps(name, shape, dt):
        return st.enter_context(nc.psum_tensor("ps_" + name, shape, dt))

    NPART = len(PARTS)
    with st:
        xs = sb("xs", [128, 10, D], F32)
        h2T = sb("h2T", [128, 8, 9 * 128], BF16)
        WA = sb("WA", [128, 12288], BF16)
        WB = sb("WB", [128, 12288], BF16)
        WC = sb("WC", [128, 14336], BF16)
        hT = [sb("hT%d" % i, [128, 8, 512], BF16) for i in range(2)]
        junk = sb("junk", [128, D], BF16)
        hb1s = [sb("hb1_%d" % i, [128, D], BF16) for i in range(2)]
        hb2 = sb("hb2", [128, D], BF16)
        csb = sb("csb", [128, 512], F32)
        ubuf = [sb("ub%d" % i, [128, 514], F32) for i in range(2)]
        tcv = sb("tcv", [128, 512], F32)
        ysb = sb("ysb", [128, 512], F32)
        ysq = sb("ysq", [128, 4, 512], BF16)
        mixc = sb("mixc", [128, 4, 512], BF16)
        mixa = sb("mixa", [128, 4, 512], BF16)
        sq = sb("sq", [128, 640], F32)
        qn = sb("qn", [128, 512], BF16)
        kn = sb("kn", [128, 128], BF16)
        qT = sb("qT", [128, 512], BF16)
        kT = [[sb("kT%d%d" % (l, p), [128, 128], BF16) for p in range(2)] for l in range(2)]
        vaug = [[sb("va%d%d" % (l, p), [128, 2, 65], BF16) for p in range(3)] for l in range(2)]
        pT = [sb("pT%d" % i, [128, 512], BF16) for i in range(4)]
        ob = sb("ob", [128, 512], BF16)
        sg = [sb("sg%d" % i, [128, 512], BF16) for i in range(2)]
        hid = sb("hid", [128, 4, 512], BF16)
        gvec = sb("gvec", [128, 72], F32)
        rowc = sb("rowc", [128, 272], F32)
        esink = sb("esink", [128, 16], F32)
        gqk = sb("gqk", [128, 128], F32)
        masks = sb("masks", [128, 3, 128], BF16)
        ident = sb("ident", [128, 128], BF16)
        ones = sb("ones", [128, 2], BF16)
        utail = [sb("utail%d" % l, [128, 4, 2], F32) for l in range(2)]
        ss1 = sb("ss1", [128, 4], F32)
        t1 = sb("t1", [128, 4], F32)
        rstd1 = sb("rstd1", [128, 4], F32)
        tc_ = sb("tc", [128, 4], F32)
        rstdc = sb("rstdc", [128, 4], F32)
        ssq = sb("ssq", [128, 10], F32)
        tq = sb("tq", [128, 10], F32)
        rq = sb("rq", [128, 10], F32)
        den = sb("den", [128, 8], F32)
        rec = sb("rec", [128, 8], F32)
        ssa = sb("ssa", [128, 1], F32)
        ta = sb("ta", [128, 1], F32)
        rstda = sb("rstda", [128, 1], F32)
        ss2 = sb("ss2", [128, 1], F32)
        t2 = sb("t2", [128, 1], F32)
        rstd2 = sb("rstd2", [128, 1], F32)

        NGEN = 8
        gens = [ps("gen%d" % i, [128, 512], F32) for i in range(NGEN)]
        gctr = [0]
        wctr = [0]

        def _last_use(key):
            v = S.last_write.get(key, -1)
            r = S.readers.get(key)
            if r:
                v = max(v, max(r))
            return v

        def gen():
            i = min(range(NGEN), key=lambda j: _last_use(("gen", j)))
            return gens[i], ("gen", i)

        def bfv(g):
            return g[:].bitcast(BF16).rearrange("p (c n) -> p c n", n=128)

        PEo = lambda fn, r, w: S.add("pe", fn, r, w)
        ACTo = lambda fn, r, w: S.add("act", fn, r, w)
        DVEo = lambda fn, r, w: S.add("dve", fn, r, w)
        POOLo = lambda fn, r, w: S.add("pool", fn, r, w)

        def rsqrt_act(src, dst, tmp, scale, rkeys, tkey, wkey):
            ACTo(lambda e: e.activation(out=tmp, in_=src, func=AF.Ln, scale=scale, bias=EPS), rkeys, [tkey])
            ACTo(lambda e: e.activation(out=dst, in_=tmp, func=AF.Exp, scale=-0.5), [tkey], [wkey])

        S.add("sp", lambda e: e.dma_start(out=gvec[:], in_=gvec_d), writes=["gvec"], dma="c0")
        S.add("sp", lambda e: e.dma_start(out=rowc[:], in_=rowc_d), writes=["rowc"], dma="c1")
        S.add("sp", lambda e: e.dma_start(out=masks[:], in_=masks_d), writes=["masks"], dma="c2")
        S.add("sp", lambda e: e.dma_start(out=ident[:], in_=ident_d), writes=["ident"], dma="c3")
        POOLo(lambda e: e.memset(ones[:], 1.0), [], ["ones"])
        for l in range(2):
            POOLo(lambda e, l=l: e.memset(utail[l][:], 0.0), [], [("utail", l, cc) for cc in range(4)])
            for p in range(3):
                POOLo(lambda e, l=l, p=p: e.memset(vaug[l][p][:], 1.0), [], [("vaug", l, p)])
            ACTo(lambda e, l=l: e.activation(out=esink[:, l * 8:(l + 1) * 8],
                                             in_=rowc[:, l * 136 + 128:l * 136 + 136], func=AF.Exp),
                 ["rowc"], [("esink", l)])
            DVEo(lambda e, l=l: e.scalar_tensor_tensor(out=gqk[:, l * 64:(l + 1) * 64],
                                                       in0=rowc[:, l * 136:l * 136 + 64], scalar=0.125,
                                                       in1=rowc[:, l * 136 + 64:l * 136 + 128],
                                                       op0=ALU.mult, op1=ALU.mult),
                 ["rowc"], [("gqk", l)])

        def load_x(b, q="sp"):
            s = slot_of(b)
            r0 = (b + NHALO) * 128
            S.add(q, lambda e: e.dma_start(out=xs[:, s, :], in_=x_d[r0:r0 + 128, :]),
                  writes=[("x", s)], dma="x%d" % s)

        for b in range(-2, 8):
            load_x(b)

        def wdma(bufname, piece, out_ap, in_ap, first):
            wr = [(bufname, piece)]
            if first:
                wr.append((bufname, "occ"))
            S.add("pool", lambda e: e.dma_start(out=out_ap, in_=in_ap), writes=wr,
                  dma="%s_%d" % (bufname, piece))

        WCo = WC[:, 0:8192].rearrange("p (k n) -> p k n", n=1024)
        WCq = WC[:, 8192:14336].rearrange("p (k n) -> p k n", n=768)

        def job_win(l, buf, bufname):
            src = w_in_d[l].rearrange("(k p) n -> p k n", p=128)
            dst = buf[:, 0:12288].rearrange("p (k n) -> p k n", n=1536)
            for kk in range(4):
                wdma(bufname, kk, dst[:, 2 * kk:2 * kk + 2, :], src[:, 2 * kk:2 * kk + 2, 0:1536], kk == 0)
            for kk in range(4):
                wdma("WC", 4 + kk, WCq[:, 2 * kk:2 * kk + 2, :], src[:, 2 * kk:2 * kk + 2, 1536:2304], kk == 0)
            srco = w_o_d[l].rearrange("(k p) n -> p k n", p=128)
            for kk in range(4):
                wdma("WC", kk, WCo[:, 2 * kk:2 * kk + 2, :], srco[:, 2 * kk:2 * kk + 2, :], False)

        def job_ffn(l, p, buf, bufname):
            c0, c1 = PARTS[p]
            n = c1 - c0
            sg_ = w_g_d[l].rearrange("(k p) n -> p k n", p=128)
            su_ = w_u_d[l].rearrange("(k p) n -> p k n", p=128)
            sd_ = w_d_d[l].rearrange("(c p) n -> p c n", p=128)
            dg = buf[:, 0:4096].rearrange("p (k n) -> p k n", n=512)
            du = buf[:, 4096:8192].rearrange("p (k n) -> p k n", n=512)
            dd = buf[:, 8192:12288].rearrange("p (c n) -> p c n", n=1024)
            for kk in range(4):
                wdma(bufname, kk, dg[:, 2 * kk:2 * kk + 2, 0:n * 128],
                     sg_[:, 2 * kk:2 * kk + 2, c0 * 128:c1 * 128], kk == 0)
            for kk in range(4):
                wdma(bufname, 4 + kk, du[:, 2 * kk:2 * kk + 2, 0:n * 128],
                     su_[:, 2 * kk:2 * kk + 2, c0 * 128:c1 * 128], False)
            for j in range(2):
                a, b_ = 2 * j, min(2 * j + 2, n)
                if a >= b_:
                    continue
                wdma(bufname, 8 + j, dd[:, a:b_, :], sd_[:, c0 + a:c0 + b_, :], False)

        phases = []
        for tile in range(2):
            for l in range(NL):
                phases.append(("M", tile, l))
                for p in range(NPART):
                    phases.append(("F", tile, l, p))
        bigbuf = [(WA, "WA"), (WB, "WB")]
        issued = [0]

        def issue_upto(n):
            while issued[0] < min(n, len(phases)):
                ph = phases[issued[0]]
                buf, bufname = bigbuf[issued[0] % 2]
                if ph[0] == "M":
                    job_win(ph[2], buf, bufname)
                else:
                    job_ffn(ph[2], ph[3], buf, bufname)
                issued[0] += 1

        def mixer(pi, tile, l, groups, fgmap):
            Wbuf, Wname = bigbuf[pi % 2]
            Wcv = Wbuf[:, 0:12288].rearrange("p (k n) -> p k n", n=1536)
            gb = l * 36
            wcv = lambda k: [(Wname, k // 2), (Wname, "occ")]
            wqk = lambda k: [("WC", 4 + k // 2), ("WC", "occ")]
            wor = lambda k: [("WC", k // 2), ("WC", "occ")]

            def make_X(gi, blks):
                gp = gi % 2
                hTg = hT[gp]
                nb = len(blks)
                N = nb * 128
                nkv = sum(1 for (_, kv) in blks if kv)
                any_full = nkv < nb
                hTr = [("hT", gp, i) for i in range(nb)]
                chunks = []

                def Apre(i, b):
                    s = slot_of(b)
                    hb1 = hb1s[i % 2]
                    ACTo(lambda e: e.activation(out=junk[:], in_=xs[:, s, :], func=AF.Square,
                                                accum_out=ss1[:, i:i + 1]), [("x", s)], [("ss1", i)])
                    rsqrt_act(ss1[:, i:i + 1], rstd1[:, i:i + 1], t1[:, i:i + 1], 1.0 / D,
                              [("ss1", i)], ("t1", i), ("rstd1", i))
                    ACTo(lambda e: e.activation(out=hb1[:], in_=xs[:, s, :], func=AF.Copy,
                                                scale=rstd1[:, i:i + 1]), [("x", s), ("rstd1", i)], [("hb1", i % 2)])

                def Ape(i, b):
                    hb1 = hb1s[i % 2]
                    g, gk = gen()
                    gv = bfv(g)
                    for k in range(8):
                        PEo(lambda e, k=k: e.transpose(out=gv[:, k, :], in_=hb1[:, k * 128:(k + 1) * 128],
                                                       identity=ident[:]), [("hb1", i % 2), "ident"], [gk])
                    DVEo(lambda e: e.tensor_tensor(
                        out=hTg[:, :, i * 128:(i + 1) * 128], in0=gv,
                        in1=gvec[:, gb:gb + 8].unsqueeze(2).to_broadcast([128, 8, 128]), op=ALU.mult),
                         [gk, "gvec"], [("hT", gp, i)])

                def B1(cc):
                    ub = ubuf[cc % 2]
                    ubk = ("ub", cc % 2)
                    pc, pck = gen()
                    for k in range(8):
                        PEo(lambda e, k=k: e.matmul(
                            pc[:, 0:N], lhsT=Wcv[:, k, 512 + cc * 128:512 + (cc + 1) * 128], rhs=hTg[:, k, 0:N],
                            start=(k == 0), stop=(k == 7)), wcv(k) + hTr, [pck])
                    phc, phck = gen()
                    for k in range(8):
                        PEo(lambda e, k=k: e.matmul(
                            phc[:, 0:N], lhsT=Wcv[:, k, 1024 + cc * 128:1024 + (cc + 1) * 128], rhs=hTg[:, k, 0:N],
                            start=(k == 0), stop=(k == 7)), wcv(k) + hTr, [phck])
                    ACTo(lambda e: e.activation(out=csb[:, 0:N], in_=pc[:, 0:N], func=AF.Copy), [pck], ["csb"])
                    POOLo(lambda e: e.tensor_copy(out=ub[:, 0:2], in_=utail[l][:, cc, :]),
                          [("utail", l, cc)], [ubk])
                    DVEo(lambda e: e.tensor_tensor(out=ub[:, 2:2 + N], in0=phc[:, 0:N], in1=csb[:, 0:N],
                                                   op=ALU.mult), [phck, "csb"], [ubk])
                    POOLo(lambda e: e.tensor_copy(out=utail[l][:, cc, :], in_=ub[:, N:N + 2]),
                          [ubk], [("utail", l, cc)])

                def B2(cc):
                    ub = ubuf[cc % 2]
                    ubk = ("ub", cc % 2)
                    pb, pbk = gen()
                    for k in range(8):
                        PEo(lambda e, k=k: e.matmul(
                            pb[:, 0:N], lhsT=Wcv[:, k, cc * 128:(cc + 1) * 128], rhs=hTg[:, k, 0:N],
                            start=(k == 0), stop=(k == 7)), wcv(k) + hTr, [pbk])
                    cw = gb + 24
                    DVEo(lambda e: e.tensor_scalar(
                        out=tcv[:, 0:N], in0=ub[:, 2:2 + N], scalar1=gvec[:, cw + 8 + cc:cw + 9 + cc], scalar2=None,
                        op0=ALU.mult), [ubk, "gvec"], ["tcv"])
                    DVEo(lambda e: e.scalar_tensor_tensor(
                        out=tcv[:, 0:N], in0=ub[:, 1:1 + N], scalar=gvec[:, cw + 4 + cc:cw + 5 + cc], in1=tcv[:, 0:N],
                        op0=ALU.mult, op1=ALU.add), [ubk, "gvec", "tcv"], ["tcv"])
                    DVEo(lambda e: e.scalar_tensor_tensor(
                        out=tcv[:, 0:N], in0=ub[:, 0:N], scalar=gvec[:, cw + cc:cw + 1 + cc], in1=tcv[:, 0:N],
                        op0=ALU.mult, op1=ALU.add), [ubk, "gvec", "tcv"], ["tcv"])
                    DVEo(lambda e: e.tensor_tensor(out=ysb[:, 0:N], in0=pb[:, 0:N], in1=tcv[:, 0:N],
                                                   op=ALU.mult), [pbk, "tcv"], ["ysb"])
                    ACTo(lambda e: e.activation(out=ysq[:, cc, 0:N], in_=ysb[:, 0:N], func=AF.Square),
                         ["ysb"], [("ysq", cc)])
                    ACTo(lambda e: e.activation(out=mixc[:, cc, 0:N], in_=ysb[:, 0:N], func=AF.Copy,
                                                scale=gvec[:, gb + 20 + cc:gb + 21 + cc]),
                         ["ysb", "gvec"], [("mixc", cc)])

                def Bfin():
                    pss, pssk = gen()
                    for i, (b, kv) in enumerate(blks):
                        if kv:
                            continue
                        for cc in range(4):
                            PEo(lambda e, i=i, cc=cc: e.matmul(
                                pss[:, i:i + 1], lhsT=ysq[:, cc, i * 128:(i + 1) * 128], rhs=ones[:, 0:1],
                                start=(cc == 0), stop=(cc == 3)), [("ysq", cc), "ones"], [pssk])
                    rsqrt_act(pss[:, nkv:nb], rstdc[:, nkv:nb], tc_[:, nkv:nb], 1.0 / 512, [pssk], "tc", "rstdc")

                def WoC(i, b):
                    s = slot_of(b)
                    tk = i * 128
                    for hf in range(2):
                        pcw, pcwk = gen()
                        for cc in range(4):
                            PEo(lambda e, cc=cc, hf=hf, pcw=pcw: e.matmul(
                                pcw[:, :], lhsT=mixc[:, cc, tk:tk + 128], rhs=WCo[:, cc, hf * 512:(hf + 1) * 512],
                                start=(cc == 0), stop=(cc == 3)), [("mixc", cc)] + wor(cc), [pcwk])
                        DVEo(lambda e, hf=hf, pcw=pcw: e.scalar_tensor_tensor(
                            out=xs[:, s, hf * 512:(hf + 1) * 512], in0=pcw[:, :], scalar=rstdc[:, i:i + 1],
                            in1=xs[:, s, hf * 512:(hf + 1) * 512], op0=ALU.mult, op1=ALU.add),
                             [pcwk, "rstdc", ("x", s)], [("x", s)])

                def Achunk(j):
                    if 0 <= j - 2 < nb:
                        Ape(j - 2, blks[j - 2][0])
                    if j < nb:
                        Apre(j, blks[j][0])

                def A01():
                    Achunk(0)
                    Achunk(1)

                chunks.append(A01)
                for j in range(2, nb + 2):
                    chunks.append(lambda j=j: Achunk(j))
                for cc in range(4):
                    chunks.append(lambda cc=cc: B1(cc))
                    if any_full:
                        chunks.append(lambda cc=cc: B2(cc))
                if any_full:
                    chunks.append(Bfin)
                    for i, (b, kv) in enumerate(blks):
                        if not kv:
                            chunks.append(lambda i=i, b=b: WoC(i, b))
                return chunks

            class Blk:
                pass

            def mk(gi, i, b, kv):
                B = Blk()
                B.gi, B.i, B.b, B.kv = gi, i, b, kv
                return B

            def st1(B):
                    gi, i, b, kv = B.gi, B.i, B.b, B.kv
                    gp = gi % 2
                    hTg = hT[gp]
                    s = slot_of(b)
                    par = b % 2
                    tk = i * 128
                    pkv, pkvk = gen()
                    for k in range(8):
                        PEo(lambda e, k=k: e.matmul(
                            pkv[:, 0:256], lhsT=hTg[:, k, tk:tk + 128], rhs=WCq[:, k, 512:768],
                            start=(k == 0), stop=(k == 7)), wqk(k) + [("hT", gp, i)], [pkvk])
                    if not kv:
                        pq, pqk = gen()
                        for k in range(8):
                            PEo(lambda e, k=k: e.matmul(
                                pq[:, :], lhsT=hTg[:, k, tk:tk + 128], rhs=WCq[:, k, 0:512],
                                start=(k == 0), stop=(k == 7)), wqk(k) + [("hT", gp, i)], [pqk])
                    ACTo(lambda e: e.activation(out=sq[:, 512:640], in_=pkv[:, 0:128], func=AF.Square),
                         [pkvk], [("sq", 1)])
                    if not kv:
                        ACTo(lambda e: e.activation(out=sq[:, 0:512], in_=pq[:, :], func=AF.Square),
                             [pqk], [("sq", 0)])
                    B.pkv, B.pkvk = pkv, pkvk
                    B.pq = None if kv else pq
                    B.pqk = None if kv else pqk

            def st1b(B):
                    gi, i, b, kv = B.gi, B.i, B.b, B.kv
                    pkv, pkvk, pq, pqk = B.pkv, B.pkvk, B.pq, B.pqk
                    lo = 8 if kv else 0
                    DVEo(lambda e: e.tensor_reduce(
                        out=ssq[:, lo:10], in_=sq[:, lo * 64:640].rearrange("p (h d) -> p h d", d=64),
                        axis=AX.X, op=ALU.add), [("sq", 1)] + ([] if kv else [("sq", 0)]), ["ssq"])
                    rsqrt_act(ssq[:, lo:10], rq[:, lo:10], tq[:, lo:10], 1.0 / 64, ["ssq"], "tq", "rq")
                    if not kv:
                        DVEo(lambda e: e.tensor_tensor(
                            out=qn[:].rearrange("p (j g d) -> p g j d", j=4, g=2),
                            in0=pq[:, :].rearrange("p (g j d) -> p g j d", g=2, j=4),
                            in1=rq[:, 0:8].rearrange("p (g j) -> p g j", g=2).unsqueeze(3).to_broadcast([128, 2, 4, 64]),
                            op=ALU.mult), [pqk, "rq"], ["qn"])
                    for h in range(2):
                        DVEo(lambda e, h=h: e.scalar_tensor_tensor(
                            out=kn[:, h * 64:(h + 1) * 64], in0=pkv[:, h * 64:(h + 1) * 64],
                            scalar=rq[:, 8 + h:9 + h], in1=gqk[:, l * 64:(l + 1) * 64],
                            op0=ALU.mult, op1=ALU.mult), [pkvk, "rq", ("gqk", l)], [("kn", h)])
                    vp = b % 3
                    ACTo(lambda e: e.activation(
                        out=vaug[l][vp][:, :, 0:64], in_=pkv[:, 128:256].rearrange("p (h d) -> p h d", d=64),
                        func=AF.Copy), [pkvk], [("vaug", l, vp)])

            def st2(B):
                    gi, i, b, kv = B.gi, B.i, B.b, B.kv
                    par = b % 2
                    pt, ptk = gen()
                    ptv = bfv(pt)
                    if not kv:
                        for j in range(4):
                            PEo(lambda e, j=j: e.transpose(out=ptv[:, j, :], in_=qn[:, j * 128:(j + 1) * 128],
                                                           identity=ident[:]), ["qn", "ident"], [ptk])
                    PEo(lambda e: e.transpose(out=ptv[:, 4, :], in_=kn[:, :], identity=ident[:]),
                        [("kn", 0), ("kn", 1), "ident"], [ptk])
                    if not kv:
                        ACTo(lambda e: e.activation(out=qT[:].rearrange("p (j t) -> p j t", j=4),
                                                    in_=ptv[:, 0:4, :], func=AF.Copy), [ptk], ["qT"])
                    ACTo(lambda e: e.activation(out=kT[l][par][:], in_=ptv[:, 4, :], func=AF.Copy),
                         [ptk], [("kT", l, par)])

            def st3(B):
                    gi, i, b, kv = B.gi, B.i, B.b, B.kv
                    par = b % 2
                    for g in range(2):
                        for kbi, kpar in enumerate((1 - par, par)):
                            psc, psck = gen()
                            PEo(lambda e, psc=psc, g=g, kpar=kpar: e.matmul(
                                psc[:, :], lhsT=kT[l][kpar][g * 64:(g + 1) * 64, :], rhs=qT[g * 64:(g + 1) * 64, :],
                                start=True, stop=True), [("kT", l, kpar), "qT"], [psck])
                            pi_ = g * 2 + kbi
                            ACTo(lambda e, psc=psc, pi_=pi_: e.activation(out=pT[pi_][:], in_=psc[:, :], func=AF.Exp),
                                 [psck], [("pT", pi_)])
                            mi = 0 if kbi == 1 else (2 if b == 0 else 1)
                            DVEo(lambda e, pi_=pi_, mi=mi: e.tensor_tensor(
                                out=pT[pi_][:].rearrange("p (j t) -> p j t", j=4),
                                in0=pT[pi_][:].rearrange("p (j t) -> p j t", j=4),
                                in1=masks[:, mi, :].unsqueeze(1).to_broadcast([128, 4, 128]), op=ALU.mult),
                                 [("pT", pi_), "masks"], [("pT", pi_)])

            def st4(B):
                    gi, i, b, kv = B.gi, B.i, B.b, B.kv
                    par = b % 2
                    pos = []
                    for g in range(2):
                        pos.append(gen())
                        po, pok = pos[g]
                        for j in range(4):
                            for kbi, kpar in enumerate(((b - 1) % 3, b % 3)):
                                pi_ = g * 2 + kbi
                                PEo(lambda e, g=g, j=j, kbi=kbi, kpar=kpar, pi_=pi_, po=po: e.matmul(
                                    po[:, j * 65:(j + 1) * 65], lhsT=pT[pi_][:, j * 128:(j + 1) * 128],
                                    rhs=vaug[l][kpar][:, g, :], start=(kbi == 0), stop=(kbi == 1)),
                                    [("pT", pi_), ("vaug", l, kpar)], [pok])
                    povs = [pos[g][0][:, 0:260].rearrange("p (j e) -> p j e", e=65) for g in range(2)]
                    for g in range(2):
                        DVEo(lambda e, g=g: e.tensor_tensor(
                            out=den[:, g * 4:(g + 1) * 4], in0=povs[g][:, :, 64],
                            in1=esink[:, l * 8 + g * 4:l * 8 + (g + 1) * 4], op=ALU.add),
                             [pos[g][1], ("esink", l)], [("den", g)])
                    DVEo(lambda e: e.reciprocal(out=rec[:], in_=den[:]), [("den", 0), ("den", 1)], ["rec"])
                    for g in range(2):
                        DVEo(lambda e, g=g: e.tensor_tensor(
                            out=ob[:, g * 256:(g + 1) * 256].rearrange("p (j d) -> p j d", j=4),
                            in0=povs[g][:, :, 0:64],
                            in1=rec[:, g * 4:(g + 1) * 4].unsqueeze(2).to_broadcast([128, 4, 64]),
                            op=ALU.mult), [pos[g][1], "rec"], [("ob", g)])

            def st4b(B):
                    ACTo(lambda e: e.activation(out=junk[:, 0:512], in_=ob[:], func=AF.Square, accum_out=ssa[:]),
                         [("ob", 0), ("ob", 1)], ["ssa"])
                    rsqrt_act(ssa[:], rstda[:], ta[:], 1.0 / 512, ["ssa"], "ta", "rstda")

            def st5(B):
                    gi, i, b, kv = B.gi, B.i, B.b, B.kv
                    tk = i * 128
                    pto, ptok = gen()
                    ptov = bfv(pto)
                    for c in range(4):
                        PEo(lambda e, c=c: e.transpose(out=ptov[:, c, :], in_=ob[:, c * 128:(c + 1) * 128],
                                                       identity=ident[:]), [("ob", 0), ("ob", 1), "ident"], [ptok])
                    DVEo(lambda e: e.tensor_tensor(
                        out=mixa[:, :, tk:tk + 128], in0=ptov[:, 0:4, :],
                        in1=gvec[:, gb + 16:gb + 20].unsqueeze(2).to_broadcast([128, 4, 128]), op=ALU.mult),
                         [ptok, "gvec"], [("mixa", i)])

            def st6(B):
                    gi, i, b, kv = B.gi, B.i, B.b, B.kv
                    tk = i * 128
                    s = slot_of(b)
                    for hf in range(2):
                        paw, pawk = gen()
                        for c in range(4):
                            PEo(lambda e, c=c, hf=hf, paw=paw: e.matmul(
                                paw[:, :], lhsT=mixa[:, c, tk:tk + 128], rhs=WCo[:, 4 + c, hf * 512:(hf + 1) * 512],
                                start=(c == 0), stop=(c == 3)), [("mixa", i)] + wor(4 + c), [pawk])
                        DVEo(lambda e, hf=hf, paw=paw: e.scalar_tensor_tensor(
                            out=xs[:, s, hf * 512:(hf + 1) * 512], in0=paw[:, :], scalar=rstda[:, 0:1],
                            in1=xs[:, s, hf * 512:(hf + 1) * 512], op0=ALU.mult, op1=ALU.add),
                             [pawk, "rstda", ("x", s)], [("x", s)])

            def st6b(B):
                    gi, i, b, kv = B.gi, B.i, B.b, B.kv
                    s = slot_of(b)
                    ACTo(lambda e: e.activation(out=junk[:], in_=xs[:, s, :], func=AF.Square, accum_out=ss2[:]),
                         [("x", s)], ["ss2"])
                    rsqrt_act(ss2[:], rstd2[:], t2[:], 1.0 / D, ["ss2"], "t2", "rstd2")
                    ACTo(lambda e: e.activation(out=hb2[:], in_=xs[:, s, :], func=AF.Copy, scale=rstd2[:, 0:1]),
                         [("x", s), "rstd2"], ["hb2"])

            def st7(B):
                    gi, i, b, kv = B.gi, B.i, B.b, B.kv
                    hi = hidx_of(b)
                    g2, g2k = gen()
                    g2v = bfv(g2)
                    for k in range(8):
                        PEo(lambda e, k=k: e.transpose(out=g2v[:, k, :], in_=hb2[:, k * 128:(k + 1) * 128],
                                                       identity=ident[:]), ["hb2", "ident"], [g2k])
                    DVEo(lambda e: e.tensor_tensor(
                        out=h2T[:, :, hi * 128:(hi + 1) * 128], in0=g2v,
                        in1=gvec[:, gb + 8:gb + 16].unsqueeze(2).to_broadcast([128, 8, 128]), op=ALU.mult),
                         [g2k, "gvec"], [("h2T", hi)])

            Xs = [make_X(gi, blks) for gi, blks in enumerate(groups)]
            for c in Xs[0]:
                c()
            flat = [mk(gi, i, b, kv) for gi, blks in enumerate(groups) for i, (b, kv) in enumerate(blks)]
            fq = deque()
            fqF = deque()
            last_flat = {}
            for t, B in enumerate(flat):
                last_flat[B.gi] = t
            f0_queued = set()

            def fill():
                if fq:
                    fq.popleft()()
                elif fqF:
                    fqF.popleft()()

            prev = None
            pp = None
            for t, cur in enumerate(flat + [None, None]):
                for g_ in range(len(groups) - 1):
                    if g_ not in f0_queued and fgmap[g_] is not None and t >= last_flat[g_] + 3:
                        f0_queued.add(g_)
                        fqF.extend(ffn_group_chunks(pi + 1, tile, l, 0, fgmap[g_][1], False))
                        f0_done.add((tile, l, fgmap[g_][0]))
                if cur is not None and cur.i == 0:
                    while fq:
                        fq.popleft()()
                    if cur.gi + 1 < len(groups):
                        fq.extend(Xs[cur.gi + 1])
                if cur is not None:
                    st1(cur)
                if prev is not None:
                    st4(prev)
                if cur is not None:
                    st1b(cur)
                if prev is not None:
                    st4b(prev)
                fill()
                if pp is not None:
                    st7(pp)
                if prev is not None:
                    st5(prev)
                if cur is not None:
                    st2(cur)
                fill()
                if prev is not None:
                    st6(prev)
                if cur is not None and not cur.kv:
                    st3(cur)
                fill()
                if prev is not None:
                    st6b(prev)
                fill()
                pp = prev
                prev = cur if (cur is not None and not cur.kv) else None
            while fq:
                fq.popleft()()
            while fqF:
                fqF.popleft()()

        def ffn_group_chunks(pi, tile, l, p, blks, final):
            Wbuf, Wname = bigbuf[pi % 2]
            c0, c1 = PARTS[p]
            n = c1 - c0
            Wg = Wbuf[:, 0:4096].rearrange("p (k n) -> p k n", n=512)
            Wu = Wbuf[:, 4096:8192].rearrange("p (k n) -> p k n", n=512)
            Wd = Wbuf[:, 8192:12288].rearrange("p (c n) -> p c n", n=1024)
            occ = (Wname, "occ")
            nb = len(blks)
            N = nb * 128
            h0 = hidx_of(blks[0]) * 128
            h2r = [("h2T", hidx_of(b)) for b in blks]
            chunks = []

            def gateup(cl):
                pg, pgk = gen()
                for k in range(8):
                    PEo(lambda e, k=k: e.matmul(
                        pg[:, 0:N], lhsT=Wg[:, k, cl * 128:(cl + 1) * 128], rhs=h2T[:, k, h0:h0 + N],
                        start=(k == 0), stop=(k == 7)), [(Wname, k // 2), occ] + h2r, [pgk])
                pu, puk = gen()
                for k in range(8):
                    PEo(lambda e, k=k: e.matmul(
                        pu[:, 0:N], lhsT=Wu[:, k, cl * 128:(cl + 1) * 128], rhs=h2T[:, k, h0:h0 + N],
                        start=(k == 0), stop=(k == 7)), [(Wname, 4 + k // 2), occ] + h2r, [puk])
                sgi = sg[cl % 2]
                ACTo(lambda e: e.activation(out=sgi[:, 0:N], in_=pg[:, 0:N], func=AF.Silu),
                     [pgk], [("sg", cl % 2)])
                DVEo(lambda e: e.tensor_tensor(out=hid[:, cl, 0:N], in0=pu[:, 0:N], in1=sgi[:, 0:N], op=ALU.mult),
                     [puk, ("sg", cl % 2)], [("hid", cl)])

            def down(i, b, last_in_group):
                s = slot_of(b)
                tk = i * 128
                for hf in range(2):
                    pd, pdk = gen()
                    for cl in range(n):
                        PEo(lambda e, cl=cl, hf=hf, pd=pd: e.matmul(
                            pd[:, :], lhsT=hid[:, cl, tk:tk + 128], rhs=Wd[:, cl, hf * 512:(hf + 1) * 512],
                            start=(cl == 0), stop=(cl == n - 1)),
                            [("hid", cl), (Wname, 8 + cl // 2), occ], [pdk])
                    DVEo(lambda e, hf=hf, pd=pd: e.tensor_tensor(
                        out=xs[:, s, hf * 512:(hf + 1) * 512], in0=pd[:, :], in1=xs[:, s, hf * 512:(hf + 1) * 512],
                        op=ALU.add), [pdk, ("x", s)], [("x", s)])
                if final and b >= 0:
                    S.add("sp", lambda e: e.dma_start(out=out_d[b * 128:(b + 1) * 128, :], in_=xs[:, s, :]),
                          reads=[("x", s)], dma="o%d" % s)
                    if tile == 0 and last_in_group:
                        for bb in blks:
                            load_x(bb + 8)

            for cl in range(n):
                chunks.append(lambda cl=cl: gateup(cl))
            for i, b in enumerate(blks):
                chunks.append(lambda i=i, b=b: down(i, b, i == nb - 1))
            return chunks

        f0_done = set()

        def ffn(pi, tile, l, p, groups, final):
            for gidx, blks in enumerate(groups):
                if p == 0 and (tile, l, gidx) in f0_done:
                    continue
                for c in ffn_group_chunks(pi, tile, l, p, blks, final):
                    c()

        issue_upto(2)
        pi = 0
        for tile in range(2):
            for l in range(NL):
                if tile == 0 and l == 0:
                    mg = [[(-2, True), (-1, False)], [(b, False) for b in range(0, 4)],
                          [(b, False) for b in range(4, 8)]]
                    fg = [[-1], [0, 1, 2, 3], [4, 5, 6, 7]]
                elif tile == 0:
                    mg = [[(-1, True)], [(b, False) for b in range(0, 4)], [(b, False) for b in range(4, 8)]]
                    fg = [[0, 1, 2, 3], [4, 5, 6, 7]]
                else:
                    mg = [[(b, False) for b in range(8, 12)], [(b, False) for b in range(12, 16)]]
                    fg = [[8, 9, 10, 11], [12, 13, 14, 15]]
                if len(mg) == len(fg):
                    fgmap = [(gidx, blks) for gidx, blks in enumerate(fg)]
                else:
                    fgmap = [None] + [(gidx, blks) for gidx, blks in enumerate(fg)]
                issue_upto(pi + 2)
                mixer(pi, tile, l, mg, fgmap)
                pi += 1
                for p in range(NPART):
                    issue_upto(pi + 2)
                    ffn(pi, tile, l, p, fg, final=(l == NL - 1 and p == NPART - 1))
                    pi += 1
        S.emit(nc)
    return nc


def _host_inputs(x, norm1_g, w_in, conv_w, q_norm_g, k_norm_g, sinks, conv_out_g, attn_out_g, w_o,
                 norm2_g, w_gate, w_up, w_down):
    f = lambda a: np.ascontiguousarray(np.asarray(a, dtype=np.float32))
    x = f(x)
    gvec = np.zeros((128, 72), np.float32)
    rowc = np.zeros((128, 272), np.float32)
    for l in range(2):
        b = l * 36
        gvec[:, b + 0:b + 8] = f(norm1_g)[l].reshape(8, 128).T
        gvec[:, b + 8:b + 16] = f(norm2_g)[l].reshape(8, 128).T
        gvec[:, b + 16:b + 20] = f(attn_out_g)[l].reshape(4, 128).T
        gvec[:, b + 20:b + 24] = f(conv_out_g)[l].reshape(4, 128).T
        cw = f(conv_w)[l]
        for tap in range(3):
            gvec[:, b + 24 + tap * 4:b + 24 + tap * 4 + 4] = cw[tap].reshape(4, 128).T
        r = l * 136
        rowc[:, r:r + 64] = f(q_norm_g)[l][None, :]
        rowc[:, r + 64:r + 128] = f(k_norm_g)[l][None, :]
        rowc[:, r + 128:r + 136] = f(sinks)[l][None, :]
    kk = np.arange(128)[:, None]
    qq = np.arange(128)[None, :]
    m_cur = (kk <= qq).astype(np.float32)
    m_prev = (kk > qq).astype(np.float32)
    ident = np.eye(128, dtype=np.float32).astype(ml_dtypes.bfloat16)
    shared = dict(w_in=f(w_in), w_o=f(w_o), w_gate=f(w_gate), w_up=f(w_up), w_down=f(w_down),
                  gvec=gvec, rowc=rowc, ident=ident)
    in_maps = []
    for c in range(NCORES):
        bi, j = divmod(c, 4)
        start = j * TOK_PER_CORE
        xc = np.zeros(((16 + NHALO) * 128, D), np.float32)
        lo = start - NHALO * 128
        if lo >= 0:
            xc[:] = x[bi, lo:start + TOK_PER_CORE]
        else:
            xc[NHALO * 128:] = x[bi, start:start + TOK_PER_CORE]
        m = np.stack([m_cur, m_prev, m_prev if j > 0 else np.zeros_like(m_prev)], axis=1)
        d = dict(shared)
        d["x"] = xc
        d["masks"] = np.ascontiguousarray(m).astype(ml_dtypes.bfloat16)
        in_maps.append(d)
    return in_maps


_NC_CACHE = {}


def kernel(**inputs):
    NL = 2
    if NL not in _NC_CACHE:
        _NC_CACHE[NL] = build(NL)
    nc = _NC_CACHE[NL]
    in_maps = _host_inputs(**inputs)
    res = run_bass_kernel_spmd(nc, in_maps, core_ids=list(range(NCORES)))
    out = np.zeros((2, 8192, D), np.float32)
    for c in range(NCORES):
        bi, j = divmod(c, 4)
        out[bi, j * TOK_PER_CORE:(j + 1) * TOK_PER_CORE] = np.asarray(res.results[c]["out"], dtype=np.float32)
    return out
```

```python
import numpy as np
import ml_dtypes
from contextlib import ExitStack
import concourse.bass as bass
import concourse.mybir as mybir
from concourse.bass_utils import run_bass_kernel_spmd

F32 = mybir.dt.float32
BF16 = mybir.dt.bfloat16
AF = mybir.ActivationFunctionType
ALU = mybir.AluOpType
AX = mybir.AxisListType

D = 1024
INC = 2304
DFF = 2816
NCH = DFF // 128
PARTS = [(0, 4), (4, 8), (8, 12), (12, 16), (16, 19), (19, 22)]
EPS = 1e-6
NCORES = 8
TOK_PER_CORE = 2048
NHALO = 2


class Sched:
    ENGS = ("pe", "act", "dve", "pool", "sp")

    def __init__(self):
        self.ops = []
        self.last_write = {}
        self.readers = {}

    def add(self, eng, fn, reads=(), writes=(), dma=None):
        idx = len(self.ops)
        deps = set()
        for k in reads:
            w = self.last_write.get(k)
            if w is not None:
                deps.add(w)
        for k in writes:
            w = self.last_write.get(k)
            if w is not None:
                deps.add(w)
            for r in self.readers.get(k, ()):
                deps.add(r)
        deps.discard(idx)
        self.ops.append(dict(eng=eng, fn=fn, deps=deps, dma=dma, signal=False))
        for k in reads:
            self.readers.setdefault(k, []).append(idx)
        for k in writes:
            self.last_write[k] = idx
            self.readers[k] = []
        return idx

    def emit(self, nc, final_wait_eng="sp"):
        ops = self.ops
        for i, o in enumerate(ops):
            for d in o["deps"]:
                od = ops[d]
                if od["dma"] is not None:
                    continue
                if od["eng"] == "pe" and o["eng"] == "pe" and o["dma"] is None:
                    continue
                od["signal"] = True
        cnt = {e: 0 for e in self.ENGS}
        dcnt = {}
        for o in ops:
            if o["dma"] is not None:
                dcnt[o["dma"]] = dcnt.get(o["dma"], 0) + 16
                o["val"] = dcnt[o["dma"]]
            elif o["signal"]:
                cnt[o["eng"]] += 1
                o["val"] = cnt[o["eng"]]
        with ExitStack() as st:
            sems = {}
            for e in self.ENGS:
                sems[("eng", e)] = st.enter_context(nc.semaphore("s_" + e))
            for k in dcnt:
                sems[("dma", k)] = st.enter_context(nc.semaphore("d_" + str(k)))
            block = st.enter_context(nc.Block())
            per_eng = {e: [] for e in self.ENGS}
            for i, o in enumerate(ops):
                per_eng[o["eng"]].append(i)

            def run(engname, eng):
                waited = {}
                for i in per_eng[engname]:
                    o = ops[i]
                    need = {}
                    for d in o["deps"]:
                        od = ops[d]
                        if od["dma"] is not None:
                            key = ("dma", od["dma"])
                        else:
                            if od["eng"] == "pe" and engname == "pe" and o["dma"] is None:
                                continue
                            key = ("eng", od["eng"])
                        v = od["val"]
                        if need.get(key, 0) < v:
                            need[key] = v
                    for key, v in need.items():
                        if waited.get(key, 0) >= v:
                            continue
                        eng.wait_ge(sems[key], v)
                        waited[key] = v
                    ins = o["fn"](eng)
                    if o["dma"] is not None:
                        ins.then_inc(sems[("dma", o["dma"])], 16)
                    elif o["signal"]:
                        ins.then_inc(sems[("eng", engname)], 1)
                if engname == final_wait_eng:
                    for k, v in dcnt.items():
                        eng.wait_ge(sems[("dma", k)], v)
                    for e in self.ENGS:
                        if e != engname and cnt[e] > 0:
                            eng.wait_ge(sems[("eng", e)], cnt[e])

            @block.tensor
            def _(e):
                run("pe", e)

            @block.scalar
            def _(e):
                run("act", e)

            @block.vector
            def _(e):
                run("dve", e)

            @block.gpsimd
            def _(e):
                run("pool", e)

            @block.sync
            def _(e):
                run("sp", e)
        return cnt, dcnt


def slot_of(b):
    return b + 2 if b < 0 else 2 + (b % 8)


def hidx_of(b):
    return 0 if b < 0 else 1 + (b % 8)


def build(NL=2):
    from collections import deque
    nc = bass.Bass("TRN2", target_bir_lowering=False)
    x_d = nc.dram_tensor("x", [(16 + NHALO) * 128, D], F32, kind="ExternalInput").ap()
    w_in_d = nc.dram_tensor("w_in", [2, D, INC], F32, kind="ExternalInput").ap()
    w_o_d = nc.dram_tensor("w_o", [2, D, D], F32, kind="ExternalInput").ap()
    w_g_d = nc.dram_tensor("w_gate", [2, D, DFF], F32, kind="ExternalInput").ap()
    w_u_d = nc.dram_tensor("w_up", [2, D, DFF], F32, kind="ExternalInput").ap()
    w_d_d = nc.dram_tensor("w_down", [2, DFF, D], F32, kind="ExternalInput").ap()
    gvec_d = nc.dram_tensor("gvec", [128, 72], F32, kind="ExternalInput").ap()
    rowc_d = nc.dram_tensor("rowc", [128, 272], F32, kind="ExternalInput").ap()
    masks_d = nc.dram_tensor("masks", [128, 3, 128], BF16, kind="ExternalInput").ap()
    ident_d = nc.dram_tensor("ident", [128, 128], BF16, kind="ExternalInput").ap()
    out_d = nc.dram_tensor("out", [TOK_PER_CORE, D], F32, kind="ExternalOutput").ap()

    S = Sched()
    st = ExitStack()

    def sb(name, shape, dt):
        return st.enter_context(nc.sbuf_tensor("sb_" + name, shape, dt))

    def ps(name, shape, dt):
        return st.enter_context(nc.psum_tensor("ps_" + name, shape, dt))

    NPART = len(PARTS)
    with st:
        xs = sb("xs", [128, 10, D], F32)
        h2T = sb("h2T", [128, 8, 9 * 128], BF16)
        WA = sb("WA", [128, 12288], BF16)
        WB = sb("WB", [128, 12288], BF16)
        WC = sb("WC", [128, 14336], BF16)
        hT = [sb("hT%d" % i, [128, 8, 512], BF16) for i in range(2)]
        junk = sb("junk", [128, D], BF16)
        hb1s = [sb("hb1_%d" % i, [128, D], BF16) for i in range(2)]
        hb2 = sb("hb2", [128, D], BF16)
        csb = sb("csb", [128, 512], F32)
        ubuf = [sb("ub%d" % i, [128, 514], F32) for i in range(2)]
        tcv = sb("tcv", [128, 512], F32)
        ysb = sb("ysb", [128, 512], F32)
        ysq = sb("ysq", [128, 4, 512], BF16)
        mixc = sb("mixc", [128, 4, 512], BF16)
        mixa = sb("mixa", [128, 4, 512], BF16)
        sq = sb("sq", [128, 640], F32)
        qn = sb("qn", [128, 512], BF16)
        kn = sb("kn", [128, 128], BF16)
        qT = sb("qT", [128, 512], BF16)
        kT = [[sb("kT%d%d" % (l, p), [128, 128], BF16) for p in range(2)] for l in range(2)]
        vaug = [[sb("va%d%d" % (l, p), [128, 2, 65], BF16) for p in range(3)] for l in range(2)]
        pT = [sb("pT%d" % i, [128, 512], BF16) for i in range(4)]
        ob = sb("ob", [128, 512], BF16)
        sg = [sb("sg%d" % i, [128, 512], BF16) for i in range(2)]
        hid = sb("hid", [128, 4, 512], BF16)
        gvec = sb("gvec", [128, 72], F32)
        rowc = sb("rowc", [128, 272], F32)
        esink = sb("esink", [128, 16], F32)
        gqk = sb("gqk", [128, 128], F32)
        masks = sb("masks", [128, 3, 128], BF16)
        ident = sb("ident", [128, 128], BF16)
        ones = sb("ones", [128, 2], BF16)
        utail = [sb("utail%d" % l, [128, 4, 2], F32) for l in range(2)]
        ss1 = sb("ss1", [128, 4], F32)
        t1 = sb("t1", [128, 4], F32)
        rstd1 = sb("rstd1", [128, 4], F32)
        tc_ = sb("tc", [128, 4], F32)
        rstdc = sb("rstdc", [128, 4], F32)
        ssq = sb("ssq", [128, 10], F32)
        tq = sb("tq", [128, 10], F32)
        rq = sb("rq", [128, 10], F32)
        den = sb("den", [128, 8], F32)
        rec = sb("rec", [128, 8], F32)
        ssa = sb("ssa", [128, 1], F32)
        ta = sb("ta", [128, 1], F32)
        rstda = sb("rstda", [128, 1], F32)
        ss2 = sb("ss2", [128, 1], F32)
        t2 = sb("t2", [128, 1], F32)
        rstd2 = sb("rstd2", [128, 1], F32)

        NGEN = 8
        gens = [ps("gen%d" % i, [128, 512], F32) for i in range(NGEN)]
        gctr = [0]
        wctr = [0]

        def _last_use(key):
            v = S.last_write.get(key, -1)
            r = S.readers.get(key)
            if r:
                v = max(v, max(r))
            return v

        def gen():
            i = min(range(NGEN), key=lambda j: _last_use(("gen", j)))
            return gens[i], ("gen", i)

        def bfv(g):
            return g[:].bitcast(BF16).rearrange("p (c n) -> p c n", n=128)

        PEo = lambda fn, r, w: S.add("pe", fn, r, w)
        ACTo = lambda fn, r, w: S.add("act", fn, r, w)
        DVEo = lambda fn, r, w: S.add("dve", fn, r, w)
        POOLo = lambda fn, r, w: S.add("pool", fn, r, w)

        def rsqrt_act(src, dst, tmp, scale, rkeys, tkey, wkey):
            ACTo(lambda e: e.activation(out=tmp, in_=src, func=AF.Ln, scale=scale, bias=EPS), rkeys, [tkey])
            ACTo(lambda e: e.activation(out=dst, in_=tmp, func=AF.Exp, scale=-0.5), [tkey], [wkey])

        S.add("sp", lambda e: e.dma_start(out=gvec[:], in_=gvec_d), writes=["gvec"], dma="c0")
        S.add("sp", lambda e: e.dma_start(out=rowc[:], in_=rowc_d), writes=["rowc"], dma="c1")
        S.add("sp", lambda e: e.dma_start(out=masks[:], in_=masks_d), writes=["masks"], dma="c2")
        S.add("sp", lambda e: e.dma_start(out=ident[:], in_=ident_d), writes=["ident"], dma="c3")
        POOLo(lambda e: e.memset(ones[:], 1.0), [], ["ones"])
        for l in range(2):
            POOLo(lambda e, l=l: e.memset(utail[l][:], 0.0), [], [("utail", l, cc) for cc in range(4)])
            for p in range(3):
                POOLo(lambda e, l=l, p=p: e.memset(vaug[l][p][:], 1.0), [], [("vaug", l, p)])
            ACTo(lambda e, l=l: e.activation(out=esink[:, l * 8:(l + 1) * 8],
                                             in_=rowc[:, l * 136 + 128:l * 136 + 136], func=AF.Exp),
                 ["rowc"], [("esink", l)])
            DVEo(lambda e, l=l: e.scalar_tensor_tensor(out=gqk[:, l * 64:(l + 1) * 64],
                                                       in0=rowc[:, l * 136:l * 136 + 64], scalar=0.125,
                                                       in1=rowc[:, l * 136 + 64:l * 136 + 128],
                                                       op0=ALU.mult, op1=ALU.mult),
                 ["rowc"], [("gqk", l)])

        def load_x(b, q="sp"):
            s = slot_of(b)
            r0 = (b + NHALO) * 128
            S.add(q, lambda e: e.dma_start(out=xs[:, s, :], in_=x_d[r0:r0 + 128, :]),
                  writes=[("x", s)], dma="x%d" % s)

        for b in range(-2, 8):
            load_x(b)

        def wdma(bufname, piece, out_ap, in_ap, first):
            wr = [(bufname, piece)]
            if first:
                wr.append((bufname, "occ"))
            S.add("pool", lambda e: e.dma_start(out=out_ap, in_=in_ap), writes=wr,
                  dma="%s_%d" % (bufname, piece))

        WCo = WC[:, 0:8192].rearrange("p (k n) -> p k n", n=1024)
        WCq = WC[:, 8192:14336].rearrange("p (k n) -> p k n", n=768)

        def job_win(l, buf, bufname):
            src = w_in_d[l].rearrange("(k p) n -> p k n", p=128)
            dst = buf[:, 0:12288].rearrange("p (k n) -> p k n", n=1536)
            for kk in range(4):
                wdma(bufname, kk, dst[:, 2 * kk:2 * kk + 2, :], src[:, 2 * kk:2 * kk + 2, 0:1536], kk == 0)
            for kk in range(4):
                wdma("WC", 4 + kk, WCq[:, 2 * kk:2 * kk + 2, :], src[:, 2 * kk:2 * kk + 2, 1536:2304], kk == 0)
            srco = w_o_d[l].rearrange("(k p) n -> p k n", p=128)
            for kk in range(4):
                wdma("WC", kk, WCo[:, 2 * kk:2 * kk + 2, :], srco[:, 2 * kk:2 * kk + 2, :], False)

        def job_ffn(l, p, buf, bufname):
            c0, c1 = PARTS[p]
            n = c1 - c0
            sg_ = w_g_d[l].rearrange("(k p) n -> p k n", p=128)
            su_ = w_u_d[l].rearrange("(k p) n -> p k n", p=128)
            sd_ = w_d_d[l].rearrange("(c p) n -> p c n", p=128)
            dg = buf[:, 0:4096].rearrange("p (k n) -> p k n", n=512)
            du = buf[:, 4096:8192].rearrange("p (k n) -> p k n", n=512)
            dd = buf[:, 8192:12288].rearrange("p (c n) -> p c n", n=1024)
            for kk in range(4):
                wdma(bufname, kk, dg[:, 2 * kk:2 * kk + 2, 0:n * 128],
                     sg_[:, 2 * kk:2 * kk + 2, c0 * 128:c1 * 128], kk == 0)
            for kk in range(4):
                wdma(bufname, 4 + kk, du[:, 2 * kk:2 * kk + 2, 0:n * 128],
                     su_[:, 2 * kk:2 * kk + 2, c0 * 128:c1 * 128], False)
            for j in range(2):
                a, b_ = 2 * j, min(2 * j + 2, n)
                if a >= b_:
                    continue
                wdma(bufname, 8 + j, dd[:, a:b_, :], sd_[:, c0 + a:c0 + b_, :], False)

        phases = []
        for tile in range(2):
            for l in range(NL):
                phases.append(("M", tile, l))
                for p in range(NPART):
                    phases.append(("F", tile, l, p))
        bigbuf = [(WA, "WA"), (WB, "WB")]
        issued = [0]

        def issue_upto(n):
            while issued[0] < min(n, len(phases)):
                ph = phases[issued[0]]
                buf, bufname = bigbuf[issued[0] % 2]
                if ph[0] == "M":
                    job_win(ph[2], buf, bufname)
                else:
                    job_ffn(ph[2], ph[3], buf, bufname)
                issued[0] += 1

        def mixer(pi, tile, l, groups, fgmap):
            Wbuf, Wname = bigbuf[pi % 2]
            Wcv = Wbuf[:, 0:12288].rearrange("p (k n) -> p k n", n=1536)
            gb = l * 36
            wcv = lambda k: [(Wname, k // 2), (Wname, "occ")]
            wqk = lambda k: [("WC", 4 + k // 2), ("WC", "occ")]
            wor = lambda k: [("WC", k // 2), ("WC", "occ")]

            def make_X(gi, blks):
                gp = gi % 2
                hTg = hT[gp]
                nb = len(blks)
                N = nb * 128
                nkv = sum(1 for (_, kv) in blks if kv)
                any_full = nkv < nb
                hTr = [("hT", gp, i) for i in range(nb)]
                chunks = []

                def Apre(i, b):
                    s = slot_of(b)
                    hb1 = hb1s[i % 2]
                    ACTo(lambda e: e.activation(out=junk[:], in_=xs[:, s, :], func=AF.Square,
                                                accum_out=ss1[:, i:i + 1]), [("x", s)], [("ss1", i)])
                    rsqrt_act(ss1[:, i:i + 1], rstd1[:, i:i + 1], t1[:, i:i + 1], 1.0 / D,
                              [("ss1", i)], ("t1", i), ("rstd1", i))
                    ACTo(lambda e: e.activation(out=hb1[:], in_=xs[:, s, :], func=AF.Copy,
                                                scale=rstd1[:, i:i + 1]), [("x", s), ("rstd1", i)], [("hb1", i % 2)])

                def Ape(i, b):
                    hb1 = hb1s[i % 2]
                    g, gk = gen()
                    gv = bfv(g)
                    for k in range(8):
                        PEo(lambda e, k=k: e.transpose(out=gv[:, k, :], in_=hb1[:, k * 128:(k + 1) * 128],
                                                       identity=ident[:]), [("hb1", i % 2), "ident"], [gk])
                    DVEo(lambda e: e.tensor_tensor(
                        out=hTg[:, :, i * 128:(i + 1) * 128], in0=gv,
                        in1=gvec[:, gb:gb + 8].unsqueeze(2).to_broadcast([128, 8, 128]), op=ALU.mult),
                         [gk, "gvec"], [("hT", gp, i)])

                def B1(cc):
                    ub = ubuf[cc % 2]
                    ubk = ("ub", cc % 2)
                    pc, pck = gen()
                    for k in range(8):
                        PEo(lambda e, k=k: e.matmul(
                            pc[:, 0:N], lhsT=Wcv[:, k, 512 + cc * 128:512 + (cc + 1) * 128], rhs=hTg[:, k, 0:N],
                            start=(k == 0), stop=(k == 7)), wcv(k) + hTr, [pck])
                    phc, phck = gen()
                    for k in range(8):
                        PEo(lambda e, k=k: e.matmul(
                            phc[:, 0:N], lhsT=Wcv[:, k, 1024 + cc * 128:1024 + (cc + 1) * 128], rhs=hTg[:, k, 0:N],
                            start=(k == 0), stop=(k == 7)), wcv(k) + hTr, [phck])
                    ACTo(lambda e: e.activation(out=csb[:, 0:N], in_=pc[:, 0:N], func=AF.Copy), [pck], ["csb"])
                    POOLo(lambda e: e.tensor_copy(out=ub[:, 0:2], in_=utail[l][:, cc, :]),
                          [("utail", l, cc)], [ubk])
                    DVEo(lambda e: e.tensor_tensor(out=ub[:, 2:2 + N], in0=phc[:, 0:N], in1=csb[:, 0:N],
                                                   op=ALU.mult), [phck, "csb"], [ubk])
                    POOLo(lambda e: e.tensor_copy(out=utail[l][:, cc, :], in_=ub[:, N:N + 2]),
                          [ubk], [("utail", l, cc)])

                def B2(cc):
                    ub = ubuf[cc % 2]
                    ubk = ("ub", cc % 2)
                    pb, pbk = gen()
                    for k in range(8):
                        PEo(lambda e, k=k: e.matmul(
                            pb[:, 0:N], lhsT=Wcv[:, k, cc * 128:(cc + 1) * 128], rhs=hTg[:, k, 0:N],
                            start=(k == 0), stop=(k == 7)), wcv(k) + hTr, [pbk])
                    cw = gb + 24
                    DVEo(lambda e: e.tensor_scalar(
                        out=tcv[:, 0:N], in0=ub[:, 2:2 + N], scalar1=gvec[:, cw + 8 + cc:cw + 9 + cc], scalar2=None,
                        op0=ALU.mult), [ubk, "gvec"], ["tcv"])
                    DVEo(lambda e: e.scalar_tensor_tensor(
                        out=tcv[:, 0:N], in0=ub[:, 1:1 + N], scalar=gvec[:, cw + 4 + cc:cw + 5 + cc], in1=tcv[:, 0:N],
                        op0=ALU.mult, op1=ALU.add), [ubk, "gvec", "tcv"], ["tcv"])
                    DVEo(lambda e: e.scalar_tensor_tensor(
                        out=tcv[:, 0:N], in0=ub[:, 0:N], scalar=gvec[:, cw + cc:cw + 1 + cc], in1=tcv[:, 0:N],
                        op0=ALU.mult, op1=ALU.add), [ubk, "gvec", "tcv"], ["tcv"])
                    DVEo(lambda e: e.tensor_tensor(out=ysb[:, 0:N], in0=pb[:, 0:N], in1=tcv[:, 0:N],
                                                   op=ALU.mult), [pbk, "tcv"], ["ysb"])
                    ACTo(lambda e: e.activation(out=ysq[:, cc, 0:N], in_=ysb[:, 0:N], func=AF.Square),
                         ["ysb"], [("ysq", cc)])
                    ACTo(lambda e: e.activation(out=mixc[:, cc, 0:N], in_=ysb[:, 0:N], func=AF.Copy,
                                                scale=gvec[:, gb + 20 + cc:gb + 21 + cc]),
                         ["ysb", "gvec"], [("mixc", cc)])

                def Bfin():
                    pss, pssk = gen()
                    for i, (b, kv) in enumerate(blks):
                        if kv:
                            continue
                        for cc in range(4):
                            PEo(lambda e, i=i, cc=cc: e.matmul(
                                pss[:, i:i + 1], lhsT=ysq[:, cc, i * 128:(i + 1) * 128], rhs=ones[:, 0:1],
                                start=(cc == 0), stop=(cc == 3)), [("ysq", cc), "ones"], [pssk])
                    rsqrt_act(pss[:, nkv:nb], rstdc[:, nkv:nb], tc_[:, nkv:nb], 1.0 / 512, [pssk], "tc", "rstdc")

                def WoC(i, b):
                    s = slot_of(b)
                    tk = i * 128
                    for hf in range(2):
                        pcw, pcwk = gen()
                        for cc in range(4):
                            PEo(lambda e, cc=cc, hf=hf, pcw=pcw: e.matmul(
                                pcw[:, :], lhsT=mixc[:, cc, tk:tk + 128], rhs=WCo[:, cc, hf * 512:(hf + 1) * 512],
                                start=(cc == 0), stop=(cc == 3)), [("mixc", cc)] + wor(cc), [pcwk])
                        DVEo(lambda e, hf=hf, pcw=pcw: e.scalar_tensor_tensor(
                            out=xs[:, s, hf * 512:(hf + 1) * 512], in0=pcw[:, :], scalar=rstdc[:, i:i + 1],
                            in1=xs[:, s, hf * 512:(hf + 1) * 512], op0=ALU.mult, op1=ALU.add),
                             [pcwk, "rstdc", ("x", s)], [("x", s)])

                def Achunk(j):
                    if 0 <= j - 2 < nb:
                        Ape(j - 2, blks[j - 2][0])
                    if j < nb:
                        Apre(j, blks[j][0])

                def A01():
                    Achunk(0)
                    Achunk(1)

                chunks.append(A01)
                for j in range(2, nb + 2):
                    chunks.append(lambda j=j: Achunk(j))
                for cc in range(4):
                    chunks.append(lambda cc=cc: B1(cc))
                    if any_full:
                        chunks.append(lambda cc=cc: B2(cc))
                if any_full:
                    chunks.append(Bfin)
                    for i, (b, kv) in enumerate(blks):
                        if not kv:
                            chunks.append(lambda i=i, b=b: WoC(i, b))
                return chunks

            class Blk:
                pass

            def mk(gi, i, b, kv):
                B = Blk()
                B.gi, B.i, B.b, B.kv = gi, i, b, kv
                return B

            def st1(B):
                    gi, i, b, kv = B.gi, B.i, B.b, B.kv
                    gp = gi % 2
                    hTg = hT[gp]
                    s = slot_of(b)
                    par = b % 2
                    tk = i * 128
                    pkv, pkvk = gen()
                    for k in range(8):
                        PEo(lambda e, k=k: e.matmul(
                            pkv[:, 0:256], lhsT=hTg[:, k, tk:tk + 128], rhs=WCq[:, k, 512:768],
                            start=(k == 0), stop=(k == 7)), wqk(k) + [("hT", gp, i)], [pkvk])
                    if not kv:
                        pq, pqk = gen()
                        for k in range(8):
                            PEo(lambda e, k=k: e.matmul(
                                pq[:, :], lhsT=hTg[:, k, tk:tk + 128], rhs=WCq[:, k, 0:512],
                                start=(k == 0), stop=(k == 7)), wqk(k) + [("hT", gp, i)], [pqk])
                    ACTo(lambda e: e.activation(out=sq[:, 512:640], in_=pkv[:, 0:128], func=AF.Square),
                         [pkvk], [("sq", 1)])
                    if not kv:
                        ACTo(lambda e: e.activation(out=sq[:, 0:512], in_=pq[:, :], func=AF.Square),
                             [pqk], [("sq", 0)])
                    B.pkv, B.pkvk = pkv, pkvk
                    B.pq = None if kv else pq
                    B.pqk = None if kv else pqk

            def st1b(B):
                    gi, i, b, kv = B.gi, B.i, B.b, B.kv
                    pkv, pkvk, pq, pqk = B.pkv, B.pkvk, B.pq, B.pqk
                    lo = 8 if kv else 0
                    DVEo(lambda e: e.tensor_reduce(
                        out=ssq[:, lo:10], in_=sq[:, lo * 64:640].rearrange("p (h d) -> p h d", d=64),
                        axis=AX.X, op=ALU.add), [("sq", 1)] + ([] if kv else [("sq", 0)]), ["ssq"])
                    rsqrt_act(ssq[:, lo:10], rq[:, lo:10], tq[:, lo:10], 1.0 / 64, ["ssq"], "tq", "rq")
                    if not kv:
                        DVEo(lambda e: e.tensor_tensor(
                            out=qn[:].rearrange("p (j g d) -> p g j d", j=4, g=2),
                            in0=pq[:, :].rearrange("p (g j d) -> p g j d", g=2, j=4),
                            in1=rq[:, 0:8].rearrange("p (g j) -> p g j", g=2).unsqueeze(3).to_broadcast([128, 2, 4, 64]),
                            op=ALU.mult), [pqk, "rq"], ["qn"])
                    for h in range(2):
                        DVEo(lambda e, h=h: e.scalar_tensor_tensor(
                            out=kn[:, h * 64:(h + 1) * 64], in0=pkv[:, h * 64:(h + 1) * 64],
                            scalar=rq[:, 8 + h:9 + h], in1=gqk[:, l * 64:(l + 1) * 64],
                            op0=ALU.mult, op1=ALU.mult), [pkvk, "rq", ("gqk", l)], [("kn", h)])
                    vp = b % 3
                    ACTo(lambda e: e.activation(
                        out=vaug[l][vp][:, :, 0:64], in_=pkv[:, 128:256].rearrange("p (h d) -> p h d", d=64),
                        func=AF.Copy), [pkvk], [("vaug", l, vp)])

            def st2(B):
                    gi, i, b, kv = B.gi, B.i, B.b, B.kv
                    par = b % 2
                    pt, ptk = gen()
                    ptv = bfv(pt)
                    if not kv:
                        for j in range(4):
                            PEo(lambda e, j=j: e.transpose(out=ptv[:, j, :], in_=qn[:, j * 128:(j + 1) * 128],
                                                           identity=ident[:]), ["qn", "ident"], [ptk])
                    PEo(lambda e: e.transpose(out=ptv[:, 4, :], in_=kn[:, :], identity=ident[:]),
                        [("kn", 0), ("kn", 1), "ident"], [ptk])
                    if not kv:
                        ACTo(lambda e: e.activation(out=qT[:].rearrange("p (j t) -> p j t", j=4),
                                                    in_=ptv[:, 0:4, :], func=AF.Copy), [ptk], ["qT"])
                    ACTo(lambda e: e.activation(out=kT[l][par][:], in_=ptv[:, 4, :], func=AF.Copy),
                         [ptk], [("kT", l, par)])

            def st3(B):
                    gi, i, b, kv = B.gi, B.i, B.b, B.kv
                    par = b % 2
                    for g in range(2):
                        for kbi, kpar in enumerate((1 - par, par)):
                            psc, psck = gen()
                            PEo(lambda e, psc=psc, g=g, kpar=kpar: e.matmul(
                                psc[:, :], lhsT=kT[l][kpar][g * 64:(g + 1) * 64, :], rhs=qT[g * 64:(g + 1) * 64, :],
                                start=True, stop=True), [("kT", l, kpar), "qT"], [psck])
                            pi_ = g * 2 + kbi
                            ACTo(lambda e, psc=psc, pi_=pi_: e.activation(out=pT[pi_][:], in_=psc[:, :], func=AF.Exp),
                                 [psck], [("pT", pi_)])
                            mi = 0 if kbi == 1 else (2 if b == 0 else 1)
                            DVEo(lambda e, pi_=pi_, mi=mi: e.tensor_tensor(
                                out=pT[pi_][:].rearrange("p (j t) -> p j t", j=4),
                                in0=pT[pi_][:].rearrange("p (j t) -> p j t", j=4),
                                in1=masks[:, mi, :].unsqueeze(1).to_broadcast([128, 4, 128]), op=ALU.mult),
                                 [("pT", pi_), "masks"], [("pT", pi_)])

            def st4(B):
                    gi, i, b, kv = B.gi, B.i, B.b, B.kv
                    par = b % 2
                    pos = []
                    for g in range(2):
                        pos.append(gen())
                        po, pok = pos[g]
                        for j in range(4):
                            for kbi, kpar in enumerate(((b - 1) % 3, b % 3)):
                                pi_ = g * 2 + kbi
                                PEo(lambda e, g=g, j=j, kbi=kbi, kpar=kpar, pi_=pi_, po=po: e.matmul(
                                    po[:, j * 65:(j + 1) * 65], lhsT=pT[pi_][:, j * 128:(j + 1) * 128],
                                    rhs=vaug[l][kpar][:, g, :], start=(kbi == 0), stop=(kbi == 1)),
                                    [("pT", pi_), ("vaug", l, kpar)], [pok])
                    povs = [pos[g][0][:, 0:260].rearrange("p (j e) -> p j e", e=65) for g in range(2)]
                    for g in range(2):
                        DVEo(lambda e, g=g: e.tensor_tensor(
                            out=den[:, g * 4:(g + 1) * 4], in0=povs[g][:, :, 64],
                            in1=esink[:, l * 8 + g * 4:l * 8 + (g + 1) * 4], op=ALU.add),
                             [pos[g][1], ("esink", l)], [("den", g)])
                    DVEo(lambda e: e.reciprocal(out=rec[:], in_=den[:]), [("den", 0), ("den", 1)], ["rec"])
                    for g in range(2):
                        DVEo(lambda e, g=g: e.tensor_tensor(
                            out=ob[:, g * 256:(g + 1) * 256].rearrange("p (j d) -> p j d", j=4),
                            in0=povs[g][:, :, 0:64],
                            in1=rec[:, g * 4:(g + 1) * 4].unsqueeze(2).to_broadcast([128, 4, 64]),
                            op=ALU.mult), [pos[g][1], "rec"], [("ob", g)])

            def st4b(B):
                    ACTo(lambda e: e.activation(out=junk[:, 0:512], in_=ob[:], func=AF.Square, accum_out=ssa[:]),
                         [("ob", 0), ("ob", 1)], ["ssa"])
                    rsqrt_act(ssa[:], rstda[:], ta[:], 1.0 / 512, ["ssa"], "ta", "rstda")

            def st5(B):
                    gi, i, b, kv = B.gi, B.i, B.b, B.kv
                    tk = i * 128
                    pto, ptok = gen()
                    ptov = bfv(pto)
                    for c in range(4):
                        PEo(lambda e, c=c: e.transpose(out=ptov[:, c, :], in_=ob[:, c * 128:(c + 1) * 128],
                                                       identity=ident[:]), [("ob", 0), ("ob", 1), "ident"], [ptok])
                    DVEo(lambda e: e.tensor_tensor(
                        out=mixa[:, :, tk:tk + 128], in0=ptov[:, 0:4, :],
                        in1=gvec[:, gb + 16:gb + 20].unsqueeze(2).to_broadcast([128, 4, 128]), op=ALU.mult),
                         [ptok, "gvec"], [("mixa", i)])

            def st6(B):
                    gi, i, b, kv = B.gi, B.i, B.b, B.kv
                    tk = i * 128
                    s = slot_of(b)
                    for hf in range(2):
                        paw, pawk = gen()
                        for c in range(4):
                            PEo(lambda e, c=c, hf=hf, paw=paw: e.matmul(
                                paw[:, :], lhsT=mixa[:, c, tk:tk + 128], rhs=WCo[:, 4 + c, hf * 512:(hf + 1) * 512],
                                start=(c == 0), stop=(c == 3)), [("mixa", i)] + wor(4 + c), [pawk])
                        DVEo(lambda e, hf=hf, paw=paw: e.scalar_tensor_tensor(
                            out=xs[:, s, hf * 512:(hf + 1) * 512], in0=paw[:, :], scalar=rstda[:, 0:1],
                            in1=xs[:, s, hf * 512:(hf + 1) * 512], op0=ALU.mult, op1=ALU.add),
                             [pawk, "rstda", ("x", s)], [("x", s)])

            def st6b(B):
                    gi, i, b, kv = B.gi, B.i, B.b, B.kv
                    s = slot_of(b)
                    ACTo(lambda e: e.activation(out=junk[:], in_=xs[:, s, :], func=AF.Square, accum_out=ss2[:]),
                         [("x", s)], ["ss2"])
                    rsqrt_act(ss2[:], rstd2[:], t2[:], 1.0 / D, ["ss2"], "t2", "rstd2")
                    ACTo(lambda e: e.activation(out=hb2[:], in_=xs[:, s, :], func=AF.Copy, scale=rstd2[:, 0:1]),
                         [("x", s), "rstd2"], ["hb2"])

            def st7(B):
                    gi, i, b, kv = B.gi, B.i, B.b, B.kv
                    hi = hidx_of(b)
                    g2, g2k = gen()
                    g2v = bfv(g2)
                    for k in range(8):
                        PEo(lambda e, k=k: e.transpose(out=g2v[:, k, :], in_=hb2[:, k * 128:(k + 1) * 128],
                                                       identity=ident[:]), ["hb2", "ident"], [g2k])
                    DVEo(lambda e: e.tensor_tensor(
                        out=h2T[:, :, hi * 128:(hi + 1) * 128], in0=g2v,
                        in1=gvec[:, gb + 8:gb + 16].unsqueeze(2).to_broadcast([128, 8, 128]), op=ALU.mult),
                         [g2k, "gvec"], [("h2T", hi)])

            Xs = [make_X(gi, blks) for gi, blks in enumerate(groups)]
            nA0 = 1 + len(groups[0])
            a1 = deque()
            if len(groups) > 1:
                nA1 = 1 + len(groups[1])
                a1.extend(Xs[1][:nA1])
                Xs[1] = Xs[1][nA1:]
            for c in Xs[0][:nA0]:
                c()
            for c in Xs[0][nA0:]:
                c()
                if a1:
                    a1.popleft()()
            while a1:
                a1.popleft()()
            flat = [mk(gi, i, b, kv) for gi, blks in enumerate(groups) for i, (b, kv) in enumerate(blks)]
            fq = deque()
            fqF = deque()
            last_flat = {}
            for t, B in enumerate(flat):
                last_flat[B.gi] = t
            f0_queued = set()

            def fill():
                if fq:
                    fq.popleft()()
                elif fqF:
                    fqF.popleft()()

            prev = None
            pp = None
            for t, cur in enumerate(flat + [None, None]):
                for g_ in range(len(groups) - 1):
                    if g_ not in f0_queued and fgmap[g_] is not None and t >= last_flat[g_] + 3:
                        f0_queued.add(g_)
                        fqF.extend(ffn_group_chunks(pi + 1, tile, l, 0, fgmap[g_][1], False))
                        f0_done.add((tile, l, fgmap[g_][0]))
                if cur is not None and cur.i == 0:
                    while fq:
                        fq.popleft()()
                    if cur.gi + 1 < len(groups):
                        fq.extend(Xs[cur.gi + 1])
                if cur is not None:
                    st1(cur)
                if prev is not None:
                    st4(prev)
                if cur is not None:
                    st1b(cur)
                if prev is not None:
                    st4b(prev)
                fill()
                if pp is not None:
                    st7(pp)
                if prev is not None:
                    st5(prev)
                if cur is not None:
                    st2(cur)
                fill()
                if prev is not None:
                    st6(prev)
                if cur is not None and not cur.kv:
                    st3(cur)
                fill()
                if prev is not None:
                    st6b(prev)
                fill()
                pp = prev
                prev = cur if (cur is not None and not cur.kv) else None
            while fq:
                fq.popleft()()
            while fqF:
                fqF.popleft()()

        def ffn_group_chunks(pi, tile, l, p, blks, final):
            Wbuf, Wname = bigbuf[pi % 2]
            c0, c1 = PARTS[p]
            n = c1 - c0
            Wg = Wbuf[:, 0:4096].rearrange("p (k n) -> p k n", n=512)
            Wu = Wbuf[:, 4096:8192].rearrange("p (k n) -> p k n", n=512)
            Wd = Wbuf[:, 8192:12288].rearrange("p (c n) -> p c n", n=1024)
            occ = (Wname, "occ")
            nb = len(blks)
            N = nb * 128
            h0 = hidx_of(blks[0]) * 128
            h2r = [("h2T", hidx_of(b)) for b in blks]
            chunks = []

            def gateup(cl):
                pg, pgk = gen()
                for k in range(8):
                    PEo(lambda e, k=k: e.matmul(
                        pg[:, 0:N], lhsT=Wg[:, k, cl * 128:(cl + 1) * 128], rhs=h2T[:, k, h0:h0 + N],
                        start=(k == 0), stop=(k == 7)), [(Wname, k // 2), occ] + h2r, [pgk])
                pu, puk = gen()
                for k in range(8):
                    PEo(lambda e, k=k: e.matmul(
                        pu[:, 0:N], lhsT=Wu[:, k, cl * 128:(cl + 1) * 128], rhs=h2T[:, k, h0:h0 + N],
                        start=(k == 0), stop=(k == 7)), [(Wname, 4 + k // 2), occ] + h2r, [puk])
                sgi = sg[cl % 2]
                ACTo(lambda e: e.activation(out=sgi[:, 0:N], in_=pg[:, 0:N], func=AF.Silu),
                     [pgk], [("sg", cl % 2)])
                DVEo(lambda e: e.tensor_tensor(out=hid[:, cl, 0:N], in0=pu[:, 0:N], in1=sgi[:, 0:N], op=ALU.mult),
                     [puk, ("sg", cl % 2)], [("hid", cl)])

            def down(i, b, last_in_group):
                s = slot_of(b)
                tk = i * 128
                for hf in range(2):
                    pd, pdk = gen()
                    for cl in range(n):
                        PEo(lambda e, cl=cl, hf=hf, pd=pd: e.matmul(
                            pd[:, :], lhsT=hid[:, cl, tk:tk + 128], rhs=Wd[:, cl, hf * 512:(hf + 1) * 512],
                            start=(cl == 0), stop=(cl == n - 1)),
                            [("hid", cl), (Wname, 8 + cl // 2), occ], [pdk])
                    DVEo(lambda e, hf=hf, pd=pd: e.tensor_tensor(
                        out=xs[:, s, hf * 512:(hf + 1) * 512], in0=pd[:, :], in1=xs[:, s, hf * 512:(hf + 1) * 512],
                        op=ALU.add), [pdk, ("x", s)], [("x", s)])
                if final and b >= 0:
                    S.add("sp", lambda e: e.dma_start(out=out_d[b * 128:(b + 1) * 128, :], in_=xs[:, s, :]),
                          reads=[("x", s)], dma="o%d" % s)
                    if tile == 0 and last_in_group:
                        for bb in blks:
                            load_x(bb + 8)

            for cl in range(n):
                chunks.append(lambda cl=cl: gateup(cl))
            for i, b in enumerate(blks):
                chunks.append(lambda i=i, b=b: down(i, b, i == nb - 1))
            return chunks

        f0_done = set()

        def ffn(pi, tile, l, p, groups, final):
            for gidx, blks in enumerate(groups):
                if p == 0 and (tile, l, gidx) in f0_done:
                    continue
                for c in ffn_group_chunks(pi, tile, l, p, blks, final):
                    c()

        issue_upto(2)
        pi = 0
        for tile in range(2):
            for l in range(NL):
                if tile == 0 and l == 0:
                    mg = [[(-2, True), (-1, False)], [(b, False) for b in range(0, 4)],
                          [(b, False) for b in range(4, 8)]]
                    fg = [[-1], [0, 1, 2, 3], [4, 5, 6, 7]]
                elif tile == 0:
                    mg = [[(-1, True)], [(b, False) for b in range(0, 4)], [(b, False) for b in range(4, 8)]]
                    fg = [[0, 1, 2, 3], [4, 5, 6, 7]]
                else:
                    mg = [[(b, False) for b in range(8, 12)], [(b, False) for b in range(12, 16)]]
                    fg = [[8, 9, 10, 11], [12, 13, 14, 15]]
                if len(mg) == len(fg):
                    fgmap = [(gidx, blks) for gidx, blks in enumerate(fg)]
                else:
                    fgmap = [None] + [(gidx, blks) for gidx, blks in enumerate(fg)]
                issue_upto(pi + 2)
                mixer(pi, tile, l, mg, fgmap)
                pi += 1
                for p in range(NPART):
                    issue_upto(pi + 2)
                    ffn(pi, tile, l, p, fg, final=(l == NL - 1 and p == NPART - 1))
                    pi += 1
        S.emit(nc)
    return nc


def _host_inputs(x, norm1_g, w_in, conv_w, q_norm_g, k_norm_g, sinks, conv_out_g, attn_out_g, w_o,
                 norm2_g, w_gate, w_up, w_down):
    f = lambda a: np.ascontiguousarray(np.asarray(a, dtype=np.float32))
    x = f(x)
    gvec = np.zeros((128, 72), np.float32)
    rowc = np.zeros((128, 272), np.float32)
    for l in range(2):
        b = l * 36
        gvec[:, b + 0:b + 8] = f(norm1_g)[l].reshape(8, 128).T
        gvec[:, b + 8:b + 16] = f(norm2_g)[l].reshape(8, 128).T
        gvec[:, b + 16:b + 20] = f(attn_out_g)[l].reshape(4, 128).T
        gvec[:, b + 20:b + 24] = f(conv_out_g)[l].reshape(4, 128).T
        cw = f(conv_w)[l]
        for tap in range(3):
            gvec[:, b + 24 + tap * 4:b + 24 + tap * 4 + 4] = cw[tap].reshape(4, 128).T
        r = l * 136
        rowc[:, r:r + 64] = f(q_norm_g)[l][None, :]
        rowc[:, r + 64:r + 128] = f(k_norm_g)[l][None, :]
        rowc[:, r + 128:r + 136] = f(sinks)[l][None, :]
    kk = np.arange(128)[:, None]
    qq = np.arange(128)[None, :]
    m_cur = (kk <= qq).astype(np.float32)
    m_prev = (kk > qq).astype(np.float32)
    ident = np.eye(128, dtype=np.float32).astype(ml_dtypes.bfloat16)
    shared = dict(w_in=f(w_in), w_o=f(w_o), w_gate=f(w_gate), w_up=f(w_up), w_down=f(w_down),
                  gvec=gvec, rowc=rowc, ident=ident)
    in_maps = []
    for c in range(NCORES):
        bi, j = divmod(c, 4)
        start = j * TOK_PER_CORE
        xc = np.zeros(((16 + NHALO) * 128, D), np.float32)
        lo = start - NHALO * 128
        if lo >= 0:
            xc[:] = x[bi, lo:start + TOK_PER_CORE]
        else:
            xc[NHALO * 128:] = x[bi, start:start + TOK_PER_CORE]
        m = np.stack([m_cur, m_prev, m_prev if j > 0 else np.zeros_like(m_prev)], axis=1)
        d = dict(shared)
        d["x"] = xc
        d["masks"] = np.ascontiguousarray(m).astype(ml_dtypes.bfloat16)
        in_maps.append(d)
    return in_maps


_NC_CACHE = {}


def kernel(**inputs):
    NL = 2
    if NL not in _NC_CACHE:
        _NC_CACHE[NL] = build(NL)
    nc = _NC_CACHE[NL]
    in_maps = _host_inputs(**inputs)
    res = run_bass_kernel_spmd(nc, in_maps, core_ids=list(range(NCORES)))
    out = np.zeros((2, 8192, D), np.float32)
    for c in range(NCORES):
        bi, j = divmod(c, 4)
        out[bi, j * TOK_PER_CORE:(j + 1) * TOK_PER_CORE] = np.asarray(res.results[c]["out"], dtype=np.float32)
    return out
```
